# Optimizing a Trainium2 kernel written in Bass

```python
import jax, jax.numpy as jnp
from jax import lax
import numpy as np

D_MODEL = 2048
BATCH = 8
SEQ = 4096
DEPTH = 4

CHUNK = 64
Q_BLOCK = 128
HEAD_DIM = 128
MEM_TOKENS = 256
MEM_HEADS = 4
MEM_W = MEM_HEADS * HEAD_DIM
MIX_W = D_MODEL
MAIN_W = MIX_W - MEM_W
GLA_HEADS = 4
GLA_KEY = MAIN_W // 2
GLA_HK = GLA_KEY // GLA_HEADS
GLA_HV = MAIN_W // GLA_HEADS
GLA_GATE_RANK = 16
GLA_GATE_NORMALIZER = 16.0
FOX_HEADS = MAIN_W // HEAD_DIM
D_FF = 256 * ((8 * D_MODEL // 3 + 255) // 256)
A_IN = 2 * GLA_KEY + MAIN_W + GLA_GATE_RANK + MAIN_W + MEM_W
B_IN = 2 * MAIN_W + MEM_W
KV_SHARED = 2 * MAIN_W + FOX_HEADS
EPS = 1e-6

kernel_name = "yoco_gla_fox_macaron_memory_trunk"


def _rmsnorm(x, g):
    xf = x.astype(jnp.float32)
    y = xf * lax.rsqrt(jnp.mean(xf * xf, axis=-1, keepdims=True) + EPS)
    return (y * g.astype(jnp.float32)).astype(x.dtype)


def _swiglu(h, w1, w3, w2):
    return (jax.nn.silu(h @ w1) * (h @ w3)) @ w2


def _gla_chunk_causal(q, k, v, log_a):
    b, s, h, dk = q.shape
    dv = v.shape[-1]
    n = s // CHUNK

    def to_chunks(t):
        return jnp.moveaxis(t.reshape(b, n, CHUNK, h, t.shape[-1]), 1, 0)

    qc, kc, vc = to_chunks(q), to_chunks(k), to_chunks(v)
    cum = jnp.cumsum(to_chunks(log_a), axis=2)
    total = cum[:, :, -1:]
    k_dec = (kc.astype(jnp.float32) * jnp.exp(total - cum)).astype(k.dtype)
    a_chunk = jnp.exp(total[:, :, 0])

    def step(state, inp):
        q_c, k_c, v_c, a_c = inp
        state = a_c[..., None] * state + jnp.einsum(
            'bchk,bchv->bhkv', k_c, v_c, preferred_element_type=jnp.float32)
        out = jnp.einsum('bchk,bhkv->bchv', q_c.astype(jnp.float32), state)
        return state, out.astype(v_c.dtype)

    state0 = jnp.zeros((b, h, dk, dv), jnp.float32)
    _, o = lax.scan(step, state0, (qc, k_dec, vc, a_chunk))
    return jnp.moveaxis(o, 0, 1).reshape(b, s, h, dv)


def _forgetting_attention(q, k, v, cum_log_f):
    b, s, h, d = q.shape
    scale = d ** -0.5
    outs = []
    for i in range(s // Q_BLOCK):
        q0 = i * Q_BLOCK
        q1 = q0 + Q_BLOCK
        logits = jnp.einsum('bqhd,bkhd->bhqk', q[:, q0:q1], k[:, :q1],
                            preferred_element_type=jnp.float32) * scale
        logits = logits + cum_log_f[:, :, q0:q1, None] - cum_log_f[:, :, None, :q1]
        causal = (q0 + jnp.arange(Q_BLOCK))[:, None] >= jnp.arange(q1)[None, :]
        p = jax.nn.softmax(jnp.where(causal, logits, -jnp.inf), axis=-1)
        outs.append(jnp.einsum('bhqk,bkhd->bqhd', p.astype(v.dtype), v[:, :q1]))
    return jnp.concatenate(outs, axis=1)


def _memory_attention(qm, mk, mv):
    logits = jnp.einsum('bshd,bmhd->bhsm', qm, mk,
                        preferred_element_type=jnp.float32) * (qm.shape[-1] ** -0.5)
    p = jax.nn.softmax(logits, axis=-1)
    return jnp.einsum('bhsm,bmhd->bshd', p.astype(mv.dtype), mv)


def setup_inputs(seed: int = 0) -> dict:
    key = jax.random.key(seed)
    ks = jax.random.split(key, 32)
    f32 = jnp.float32
    n_a = DEPTH // 2
    n_b = DEPTH - n_a

    def w(k, shape, fan_in, scale=1.0):
        return jax.random.normal(k, shape, f32) * (scale * fan_in ** -0.5)

    def gain(k, shape):
        return 1.0 + 0.05 * jax.random.normal(k, shape, f32)

    return {
        "x": jax.random.normal(ks[0], (BATCH, SEQ, D_MODEL), f32),
        "mem": jax.random.normal(ks[1], (BATCH, MEM_TOKENS, D_MODEL), f32),
        "ffn_norm": gain(ks[2], (DEPTH, 2, D_MODEL)),
        "ffn_w1": w(ks[3], (DEPTH, 2, D_MODEL, D_FF), D_MODEL),
        "ffn_w3": w(ks[4], (DEPTH, 2, D_MODEL, D_FF), D_MODEL),
        "ffn_w2": w(ks[5], (DEPTH, 2, D_FF, D_MODEL), D_FF),
        "mix_norm": gain(ks[6], (DEPTH, D_MODEL)),
        "mem_norm": gain(ks[7], (DEPTH, D_MODEL)),
        "w_mem_kv": w(ks[8], (DEPTH, D_MODEL, 2 * MEM_W), D_MODEL),
        "mem_q_norm": gain(ks[9], (DEPTH, HEAD_DIM)),
        "mem_k_norm": gain(ks[10], (DEPTH, HEAD_DIM)),
        "w_out": w(ks[11], (DEPTH, MIX_W, D_MODEL), MIX_W),
        "a_w_in": w(ks[12], (n_a, D_MODEL, A_IN), D_MODEL),
        "a_w_gate_up": w(ks[13], (n_a, GLA_GATE_RANK, GLA_KEY), GLA_GATE_RANK),
        "a_b_gate": 0.1 * jax.random.normal(ks[14], (n_a, GLA_KEY), f32),
        "a_out_norm": gain(ks[15], (n_a, GLA_HV)),
        "b_w_in": w(ks[16], (n_b, D_MODEL, B_IN), D_MODEL),
        "b_q_norm": gain(ks[17], (n_b, HEAD_DIM)),
        "kv_norm": gain(ks[18], (D_MODEL,)),
        "w_kv": w(ks[19], (D_MODEL, KV_SHARED), D_MODEL),
        "b_f": jax.random.uniform(ks[20], (FOX_HEADS,), f32, 1.0, 5.0),
        "k_norm": gain(ks[21], (HEAD_DIM,)),
    }


def reference(x, mem, ffn_norm, ffn_w1, ffn_w3, ffn_w2, mix_norm, mem_norm,
              w_mem_kv, mem_q_norm, mem_k_norm, w_out, a_w_in, a_w_gate_up,
              a_b_gate, a_out_norm, b_w_in, b_q_norm, kv_norm, w_kv, b_f, k_norm):
    bsz, seq, _ = x.shape
    n_a = a_w_in.shape[0]
    a_split = list(np.cumsum([GLA_KEY, GLA_KEY, MAIN_W, GLA_GATE_RANK, MAIN_W]))
    b_split = [MAIN_W, 2 * MAIN_W]
    ks_shared = vs_shared = cum_shared = None

    for l in range(DEPTH):
        if l == n_a:
            hs = _rmsnorm(x, kv_norm)
            k_s, v_s, f_s = jnp.split(hs @ w_kv, b_split, axis=-1)
            ks_shared = _rmsnorm(k_s.reshape(bsz, seq, FOX_HEADS, HEAD_DIM), k_norm)
            vs_shared = v_s.reshape(bsz, seq, FOX_HEADS, HEAD_DIM)
            log_f = jax.nn.log_sigmoid((f_s + b_f).astype(jnp.float32))
            cum_shared = jnp.moveaxis(jnp.cumsum(log_f, axis=1), 1, 2)

        x = x + 0.5 * _swiglu(_rmsnorm(x, ffn_norm[l, 0]), ffn_w1[l, 0],
                              ffn_w3[l, 0], ffn_w2[l, 0])

        h = _rmsnorm(x, mix_norm[l])
        if l < n_a:
            q, k, v, lr, g, qm = jnp.split(h @ a_w_in[l], a_split, axis=-1)
            q = q.reshape(bsz, seq, GLA_HEADS, GLA_HK) * (GLA_HK ** -0.5)
            k = k.reshape(bsz, seq, GLA_HEADS, GLA_HK)
            v = v.reshape(bsz, seq, GLA_HEADS, GLA_HV)
            log_a = jax.nn.log_sigmoid(
                (lr @ a_w_gate_up[l] + a_b_gate[l]).astype(jnp.float32)) / GLA_GATE_NORMALIZER
            log_a = log_a.reshape(bsz, seq, GLA_HEADS, GLA_HK)
            o = _gla_chunk_causal(q, k, v, log_a)
            o = _rmsnorm(o, a_out_norm[l]) * jax.nn.silu(g.reshape(bsz, seq, GLA_HEADS, GLA_HV))
        else:
            j = l - n_a
            q, g, qm = jnp.split(h @ b_w_in[j], b_split, axis=-1)
            q = _rmsnorm(q.reshape(bsz, seq, FOX_HEADS, HEAD_DIM), b_q_norm[j])
            o = _forgetting_attention(q, ks_shared, vs_shared, cum_shared)
            o = o * jax.nn.sigmoid(g.reshape(bsz, seq, FOX_HEADS, HEAD_DIM))
        o = o.reshape(bsz, seq, MAIN_W)

        mh = _rmsnorm(mem, mem_norm[l])
        mkv = (mh @ w_mem_kv[l]).reshape(bsz, mem.shape[1], 2, MEM_HEADS, HEAD_DIM)
        mk = _rmsnorm(mkv[:, :, 0], mem_k_norm[l])
        mv = mkv[:, :, 1]
        qm = _rmsnorm(qm.reshape(bsz, seq, MEM_HEADS, HEAD_DIM), mem_q_norm[l])
        mo = _memory_attention(qm, mk, mv).reshape(bsz, seq, MEM_W)

        x = x + jnp.concatenate([o, mo], axis=-1) @ w_out[l]

        x = x + 0.5 * _swiglu(_rmsnorm(x, ffn_norm[l, 1]), ffn_w1[l, 1],
                              ffn_w3[l, 1], ffn_w2[l, 1])
    return x
```

```python
import bisect
from contextlib import ExitStack

import numpy as np
import concourse.bass as bass
import concourse.mybir as mybir
from concourse.bass_utils import run_bass_kernel_spmd

F32 = mybir.dt.float32
BF16 = mybir.dt.bfloat16
AF = mybir.ActivationFunctionType
ALU = mybir.AluOpType
AX = mybir.AxisListType

D = 2048
DC = 16
T = 512
DFF = 5632
FC = 44
EPS = 1e-6
MEM = 256
GLA_K = 768
MAIN = 1536
A_IN = 5136
B_IN = 3584
KVW = 3084
DEPTH = 4

PE, ACT, DVE, POOL, SP = "pe", "act", "dve", "pool", "sp"
ENGS = (PE, ACT, DVE, POOL, SP)


class _Op:
    __slots__ = ("fn", "waits", "sig", "inc")

    def __init__(self, fn):
        self.fn = fn
        self.waits = []
        self.sig = None
        self.inc = 1


class Prog:
    def __init__(self, nc, n_dma_sems=40):
        self.nc = nc
        self.ops = {e: [] for e in ENGS}
        self.sem = {}
        self.cnt = {}
        self.sigs = {}
        for e in (PE, ACT, DVE, POOL):
            self.sem[e] = nc.alloc_semaphore("sem_" + e)
            self.cnt[e] = 0
            self.sigs[e] = ([], [])
        self.dsem = [nc.alloc_semaphore("sem_dma%d" % i) for i in range(n_dma_sems)]
        self.dcnt = [0] * n_dma_sems
        self.drr = 0
        self.waited = {e: {} for e in ENGS}
        self.tok = {}

    def _resolve(self, acc):
        if acc[0] == "d":
            return ("d%d" % acc[1], self.dsem[acc[1]], acc[2])
        _, eng, idx = acc
        idxs, vals = self.sigs[eng]
        if not idxs or idx > idxs[-1]:
            self.cnt[eng] += 1
            op = self.ops[eng][idx]
            op.sig = (self.sem[eng], self.cnt[eng])
            idxs.append(idx)
            vals.append(self.cnt[eng])
            return (eng, self.sem[eng], self.cnt[eng])
        j = bisect.bisect_left(idxs, idx)
        return (eng, self.sem[eng], vals[j])

    def _add_wait(self, eng, op, acc):
        if acc is None:
            return
        if acc[0] == "c" and acc[1] == eng and eng == PE:
            return
        key, sem, val = self._resolve(acc)
        if self.waited[eng].get(key, 0) >= val:
            return
        self.waited[eng][key] = val
        op.waits.append((sem, val))

    def emit(self, eng, fn, reads=(), writes=(), dma=False):
        op = _Op(fn)
        for t in reads:
            st = self.tok.get(t)
            if st is not None:
                self._add_wait(eng, op, st[0])
        for t in writes:
            st = self.tok.get(t)
            if st is not None:
                self._add_wait(eng, op, st[0])
                for r in st[1]:
                    self._add_wait(eng, op, r)
        idx = len(self.ops[eng])
        if dma:
            i = self.drr
            self.drr = (self.drr + 1) % len(self.dsem)
            if self.dcnt[i] > 0:
                self._add_wait(eng, op, ("d", i, self.dcnt[i]))
            self.dcnt[i] += 16
            op.sig = (self.dsem[i], self.dcnt[i])
            op.inc = 16
            acc = ("d", i, self.dcnt[i])
        else:
            acc = ("c", eng, idx)
        self.ops[eng].append(op)
        for t in reads:
            st = self.tok.setdefault(t, [None, []])
            if acc[0] == "c":
                st[1] = [r for r in st[1] if not (r[0] == "c" and r[1] == eng)]
            st[1].append(acc)
        for t in writes:
            self.tok[t] = [acc, []]
        return acc

    def wait_all(self, eng, accs):
        op = _Op(None)
        for a in accs:
            self._add_wait(eng, op, a)
        self.ops[eng].append(op)

    def run(self, eng, e):
        for op in self.ops[eng]:
            for sem, val in op.waits:
                e.wait_ge(sem, val)
            if op.fn is None:
                continue
            ins = op.fn(e)
            if op.sig is not None:
                ins.then_inc(op.sig[0], op.inc)

    def finalize(self):
        nc = self.nc
        with nc.Block() as block:
            @block.tensor
            def _(e):
                self.run(PE, e)

            @block.scalar
            def _(e):
                self.run(ACT, e)

            @block.vector
            def _(e):
                self.run(DVE, e)

            @block.gpsimd
            def _(e):
                self.run(POOL, e)

            @block.sync
            def _(e):
                self.run(SP, e)


class K:
    def __init__(self, S, layers=(0, 1, 2, 3), n_slots=4, parts=("ffn0", "mix", "ffn1")):
        self.S = S
        self.NT = S // T
        self.layers = list(layers)
        self.parts = parts
        self.NS = n_slots
        self.nc = bass.Bass("TRN2", target_bir_lowering=False)
        self.P = None
        self.es = ExitStack()

    def dram(self, name, shape, dt, kind="Internal"):
        return self.nc.dram_tensor(name, list(shape), dt, kind=kind).ap()

    def sb(self, name, shape, dt):
        return self.es.enter_context(self.nc.sbuf_tensor(name, list(shape), dt))

    def mm(self, out, lhsT, rhs, start, stop, reads, writes):
        return self.P.emit(PE, lambda e: e.matmul(out, lhsT, rhs, start=start, stop=stop), reads, writes)

    def act(self, out, in_, func, reads, writes, bias=None, scale=None):
        kw = {}
        if bias is not None:
            kw["bias"] = bias
        if scale is not None:
            kw["scale"] = scale
        return self.P.emit(ACT, lambda e: e.activation(out=out, in_=in_, func=func, **kw), reads, writes)

    def stt(self, eng, out, in0, scalar, in1, op0, op1, reads, writes):
        return self.P.emit(eng, lambda e: e.scalar_tensor_tensor(out=out, in0=in0, scalar=scalar, in1=in1,
                                                                 op0=op0, op1=op1), reads, writes)

    def ts(self, eng, out, in0, s1, s2, op0, op1, reads, writes):
        if s2 is None:
            return self.P.emit(eng, lambda e: e.tensor_scalar(out=out, in0=in0, scalar1=s1, scalar2=None, op0=op0),
                               reads, writes)
        return self.P.emit(eng, lambda e: e.tensor_scalar(out=out, in0=in0, scalar1=s1, scalar2=s2,
                                                          op0=op0, op1=op1), reads, writes)

    def tt(self, eng, out, in0, in1, op, reads, writes):
        return self.P.emit(eng, lambda e: e.tensor_tensor(out=out, in0=in0, in1=in1, op=op), reads, writes)

    def cp(self, eng, out, in_, reads, writes):
        if eng == ACT:
            return self.P.emit(ACT, lambda e: e.activation(out=out, in_=in_, func=AF.Copy), reads, writes)
        return self.P.emit(eng, lambda e: e.tensor_copy(out=out, in_=in_), reads, writes)

    def recip(self, out, in_, reads, writes):
        return self.P.emit(DVE, lambda e: e.reciprocal(out=out, in_=in_), reads, writes)

    def memset(self, eng, ap, val, writes):
        return self.P.emit(eng, lambda e: e.memset(ap, val), [], writes)

    def dma(self, eng, out, in_, reads, writes):
        return self.P.emit(eng, lambda e: e.dma_start(out=out, in_=in_), reads, writes, dma=True)

    def load_slot(self, key, r0, nk, c0, ncols):
        s = self.slot_i % self.NS
        self.slot_i += 1
        tok = ("slot", s)
        view = self.slots[:, s, 0:nk, 0:ncols]
        src = self.w_bf[key][r0:r0 + nk * 128, c0:c0 + ncols]
        self.dma(SP, view, src.rearrange("(k p) j -> p k j", p=128), self.w_tokens[key], [tok])
        return self.slots[:, s], tok

    def alloc_slot(self):
        s = self.slot_i % self.NS
        self.slot_i += 1
        return self.slots[:, s], ("slot", s)

    def proj_fm(self, slot, tok, c0, m, out_ap, out_tok, ncol=T, col0=0):
        for k in range(DC):
            self.mm(out_ap, slot[:, k, c0:c0 + m], self.ht[:, k, col0:col0 + ncol], k == 0, k == DC - 1,
                    [tok, ("ht", k)], [out_tok])

    def proj_tm(self, slot, tok, c0, n, tg, out_ap, out_tok):
        for k in range(DC):
            self.mm(out_ap, self.ht[:, k, tg * 128:(tg + 1) * 128], slot[:, k, c0:c0 + n], k == 0, k == DC - 1,
                    [tok, ("ht", k)], [out_tok])

    def sqbuf(self):
        b = self.sq_i % 2
        self.sq_i += 1
        return b

    def headnorm(self, src_ps, src_tok, gcol_ap, out_ap, out_tok, n=T):
        b = self.sqbuf()
        self.act(self.sq[:, b, 0:n], src_ps, AF.Square, [src_tok], [("sq", b)])
        self.mm(self.ps[:, 7, 0:n], self.ones[:], self.sq[:, b, 0:n], True, True, [("sq", b), "ones"], [("ps", 7)])
        self.rstd_from(self.ps[:, 7, 0:n], ("ps", 7), 1.0 / 128, rstd=self.rstd[:, 0:n])
        self.stt(DVE, out_ap, src_ps, gcol_ap, self.rstd[:, 0:n], ALU.mult, ALU.mult,
                 [src_tok, "rstd", "vecs"], [out_tok])

    def build(self):
        nc = self.nc
        S, NT = self.S, self.NT
        EI = "ExternalInput"
        self.xT = self.dram("xT", [D, S], F32, EI)
        self.memT = self.dram("memT", [D, MEM], F32, EI)
        self.outT = self.dram("outT", [D, S], F32, "ExternalOutput")
        self.vecs_d = self.dram("vecs", [128, NV], F32, EI)
        self.rows_d = self.dram("rows", [128, NR], F32, EI)
        self.wup_d = self.dram("wup", [2, 17, GLA_K], F32, EI)
        self.w_shapes = {}
        for l in self.layers:
            if "ffn0" in self.parts:
                self.w_shapes["w1_%d_0" % l] = [D, DFF]
                self.w_shapes["w3_%d_0" % l] = [D, DFF]
                self.w_shapes["w2_%d_0" % l] = [DFF, D]
            if "mix" in self.parts:
                if l == 2:
                    self.w_shapes["wkv"] = [D, KVW]
                self.w_shapes["wmem_%d" % l] = [D, 1024]
                self.w_shapes["win_%d" % l] = [D, A_IN if l < 2 else B_IN]
                self.w_shapes["wout_%d" % l] = [D, D]
            if "ffn1" in self.parts:
                self.w_shapes["w1_%d_1" % l] = [D, DFF]
                self.w_shapes["w3_%d_1" % l] = [D, DFF]
                self.w_shapes["w2_%d_1" % l] = [DFF, D]
        self.w_f32 = {}
        self.w_bf = {}
        for key, shp in self.w_shapes.items():
            self.w_f32[key] = self.dram(key, shp, F32, EI)
            self.w_bf[key] = self.dram(key + "_bf", shp, BF16)
        self.kT_d = self.dram("kT_scr", [12, 128, S], BF16)
        self.v_d = self.dram("v_scr", [S, MAIN], BF16)

        self.P = Prog(nc)
        P = self.P
        self.xt = self.sb("xt", [128, DC, T], F32)
        self.ht = self.sb("ht", [128, DC, T], BF16)
        self.gt = self.sb("gt", [128, FC, T], BF16)
        self.slots = self.sb("slots", [128, self.NS, 16, 512], BF16)
        self.vecs = self.sb("vecs_sb", [128, NV], F32)
        self.rows = self.sb("rows_sb", [128, 12], F32)
        self.ones = self.sb("ones", [128, 128], BF16)
        self.sq = self.sb("sq", [128, 2, T], BF16)
        self.rstd = self.sb("rstd", [128, T], F32)
        self.stmp = self.sb("stmp", [128, 2, T], F32)
        self.memk = self.sb("memk", [128, 4, MEM], BF16)
        self.memv = self.sb("memv", [128, 2, 512], BF16)
        self.pt = self.sb("pt", [128, 2, T], BF16)
        self.wup = self.sb("wup_sb", [32, 2, GLA_K], BF16)
        self.lrT = self.sb("lrT", [32, T], BF16)
        self.SA = self.sb("SA", [128, 4, 384], F32)
        self.SB = self.sb("SB", [64, 4, 384], F32)
        self.SAb = self.sb("SAb", [128, 2, 384], BF16)
        self.SBb = self.sb("SBb", [64, 2, 384], BF16)
        self.ef = self.sb("ef", [128, GLA_K], F32)
        self.spf = self.sb("spf", [128, GLA_K], F32)
        self.acol = self.sb("acol", [128, 8, 8], F32)
        self.mskD = self.sb("mskD", [128, 128], F32)
        self.ind2 = self.sb("ind2", [128, 2], F32)
        self.cT = self.sb("cT", [128, 32, 12], F32)
        self.cref = self.sb("cref", [128, 32, 12], F32)
        self.carry = self.sb("carry", [128, 12], F32)
        self.fz = self.sb("fz", [128, 12], F32)
        self.fsp = self.sb("fsp", [128, 12], F32)
        self.mtri = self.sb("mtri", [128, 128], F32)
        self.mones = self.sb("mones", [128, 128], F32)
        self.mhalf = self.sb("mhalf", [128, 128], F32)
        self.cmask = self.sb("cmask", [128, 128], BF16)
        self.biasb = self.sb("biasb", [128, 2, 32], F32)
        self.pb = self.sb("pb", [128, 4, 128], BF16)
        self.rcp = self.sb("rcp", [128, 2, 128], F32)
        self.ps = self.es.enter_context(nc.psum_tensor("ps", [128, 8, 512], F32))
        self.slot_i = 0
        self.sq_i = 0
        self.st_i = 0
        self.pt_i = 0
        self.pb_i = 0

        g = self.gt
        self.qT = g[:, 0:8, :]
        self.sg = g[:, 8:20, :]
        self.qmn = g[:, 20:24, :]
        self.vtm = g[:, 24:36, :]
        self.kdec = g[:, 36:42, :]
        self.Qn = g[:, 0:8, :]
        self.kst = g[:, 0:12, :]
        self.vst = g[:, 12:24, :]

        self.memset(DVE, self.ones[:], 1.0, ["ones"])
        self.dma(ACT, self.vecs[:], self.vecs_d, [], ["vecs"])
        self.dma(ACT, self.rows[:], self.rows_d[:, R_BF:R_BF + 12], [], ["rows"])
        self.dma(ACT, self.mskD[:], self.rows_d[:, R_MSKD:R_MSKD + 128], [], ["mskD"])
        self.dma(ACT, self.ind2[:], self.rows_d[:, R_IND2:R_IND2 + 2], [], ["ind2"])
        self.dma(ACT, self.mtri[:], self.rows_d[:, R_TRI:R_TRI + 128], [], ["mtri"])
        self.dma(ACT, self.mones[:], self.rows_d[:, R_ONES:R_ONES + 128], [], ["mones"])
        self.dma(ACT, self.mhalf[:], self.rows_d[:, R_HALF:R_HALF + 128], [], ["mhalf"])
        self.dma(POOL, self.cmask[:], self.rows_d[:, R_CMASK:R_CMASK + 128], [], ["cmask"])
        self.memset(DVE, self.lrT[:], 1.0, ["lrT"])
        for a in range(2):
            self.dma(POOL, self.wup[0:17, a, :], self.wup_d[a], [], ["wup"])

        RP = 256
        self.w_tokens = {}
        for key, shp in self.w_shapes.items():
            toks = []
            for r0 in range(0, shp[0], RP):
                r1 = min(shp[0], r0 + RP)
                tk = ("wb", key, r0)
                self.dma(POOL, self.w_bf[key][r0:r1, :], self.w_f32[key][r0:r1, :], [], [tk])
                toks.append(tk)
            self.w_tokens[key] = toks

        out_accs = []
        xts = [("xt", c) for c in range(DC)]
        for li, l in enumerate(self.layers):
            if "mix" in self.parts:
                self.mem_kv(l)
                if l < 2:
                    self.memset(DVE, self.SA[:], 0.0, [("SA", h) for h in range(4)])
                    self.memset(DVE, self.SB[:], 0.0, [("SB", h) for h in range(4)])
                if l == 2:
                    self.memset(DVE, self.carry[:], 0.0, ["carry"])
            for t in range(NT):
                c0 = t * T
                src = self.xT if li == 0 else self.outT
                self.dma(ACT, self.xt[:], src[:, c0:c0 + T].rearrange("(c p) s -> p c s", p=128),
                         [("xd", t)], xts)
                if "mix" in self.parts and l == 2:
                    self.kv_shared(t)
                if "ffn0" in self.parts:
                    self.ffn(l, 0)
                if "mix" in self.parts:
                    if l < 2:
                        self.mixer_a(l, t)
                    else:
                        self.mixer_b(l, t)
                    self.wout(l)
                if "ffn1" in self.parts:
                    self.ffn(l, 1)
                a = self.dma(ACT, self.outT[:, c0:c0 + T].rearrange("(c p) s -> p c s", p=128), self.xt[:],
                             xts, [("xd", t)])
                out_accs.append(a)
        P.wait_all(ACT, out_accs[-NT:])
        P.finalize()
        self.es.close()
        return nc

    def rstd_from(self, ps_ap, ps_tok, inv_n, rstd=None, tok="rstd"):
        if rstd is None:
            rstd = self.rstd[:]
        self.ts(DVE, rstd, ps_ap, inv_n, EPS, ALU.mult, ALU.add, [ps_tok], [tok])
        self.act(rstd, rstd, AF.Sqrt, [tok], [tok])
        self.recip(rstd, rstd, [tok], [tok])

    def rmsnorm_to_ht(self, gcol, n=T):
        ps7 = self.ps[:, 7, 0:n]
        for c in range(DC):
            b = self.sqbuf()
            self.act(self.sq[:, b, 0:n], self.xt[:, c, 0:n], AF.Square, [("xt", c)], [("sq", b)])
            self.mm(ps7, self.ones[:], self.sq[:, b, 0:n], c == 0, c == DC - 1, [("sq", b), "ones"], [("ps", 7)])
        self.rstd_from(ps7, ("ps", 7), 1.0 / D, rstd=self.rstd[:, 0:n])
        for c in range(DC):
            self.stt(DVE, self.ht[:, c, 0:n], self.xt[:, c, 0:n], self.vecs[:, gcol + c:gcol + c + 1],
                     self.rstd[:, 0:n], ALU.mult, ALU.mult, [("xt", c), "rstd", "vecs"], [("ht", c)])

    def ffn(self, l, i):
        self.rmsnorm_to_ht(V_FFN + (l * 2 + i) * DC)
        k1, k3, k2 = ("w1_%d_%d" % (l, i)), ("w3_%d_%d" % (l, i)), ("w2_%d_%d" % (l, i))
        j = 0
        for fb in range(FC // 4):
            s1, t1 = self.load_slot(k1, 0, 16, fb * 512, 512)
            s3, t3 = self.load_slot(k3, 0, 16, fb * 512, 512)
            for jj in range(4):
                bu = (2 * j) % 8
                bv = bu + 1
                self.proj_fm(s1, t1, jj * 128, 128, self.ps[:, bu, :], ("ps", bu))
                self.proj_fm(s3, t3, jj * 128, 128, self.ps[:, bv, :], ("ps", bv))
                b = self.st_i % 2
                self.st_i += 1
                self.act(self.stmp[:, b, :], self.ps[:, bu, :], AF.Silu, [("ps", bu)], [("stmp", b)])
                self.tt(DVE, self.gt[:, j, :], self.stmp[:, b, :], self.ps[:, bv, :], ALU.mult,
                        [("stmp", b), ("ps", bv)], [("gt", j)])
                j += 1
        for db in range(4):
            for ft in range(3):
                f0 = ft * 16
                nf = min(16, FC - f0)
                s2, t2 = self.load_slot(k2, f0 * 128, nf, db * 512, 512)
                for dc in range(4):
                    bank = (db % 2) * 4 + dc
                    for fk in range(nf):
                        f = f0 + fk
                        self.mm(self.ps[:, bank, :], s2[:, fk, dc * 128:(dc + 1) * 128], self.gt[:, f, :],
                                f == 0, f == FC - 1, [t2, ("gt", f)], [("ps", bank)])
            for dc in range(4):
                bank = (db % 2) * 4 + dc
                c = db * 4 + dc
                self.stt(DVE, self.xt[:, c, :], self.ps[:, bank, :], 0.5, self.xt[:, c, :], ALU.mult, ALU.add,
                         [("ps", bank), ("xt", c)], [("xt", c)])

    def wout(self, l):
        key = "wout_%d" % l
        for db in range(4):
            s, tk = self.load_slot(key, 0, 16, db * 512, 512)
            for dc in range(4):
                bank = (db % 2) * 4 + dc
                self.proj_fm(s, tk, dc * 128, 128, self.ps[:, bank, :], ("ps", bank))
            for dc in range(4):
                bank = (db % 2) * 4 + dc
                c = db * 4 + dc
                self.tt(DVE, self.xt[:, c, :], self.ps[:, bank, :], self.xt[:, c, :], ALU.add,
                        [("ps", bank), ("xt", c)], [("xt", c)])

    def mem_kv(self, l):
        xts = [("xt", c) for c in range(DC)]
        self.dma(ACT, self.xt[:, :, 0:MEM], self.memT.rearrange("(c p) m -> p c m", p=128), [], xts)
        self.rmsnorm_to_ht(V_MEMN + l * DC, n=MEM)
        key = "wmem_%d" % l
        sk, tk = self.load_slot(key, 0, 16, 0, 512)
        sv, tv = self.load_slot(key, 0, 16, 512, 512)
        for h in range(4):
            bank = h
            self.proj_fm(sk, tk, h * 128, 128, self.ps[:, bank, 0:MEM], ("ps", bank), ncol=MEM)
            self.headnorm(self.ps[:, bank, 0:MEM], ("ps", bank), self.vecs[:, V_MK + l:V_MK + l + 1],
                          self.memk[:, h, :], "memk", n=MEM)
        for mc in range(2):
            bank = 4 + mc
            self.proj_tm(sv, tv, 0, 512, mc, self.ps[:, bank, :], ("ps", bank))
            self.cp(ACT, self.memv[:, mc, :], self.ps[:, bank, :], [("ps", bank)], ["memv"])

    def mem_attn(self, l, slot, tok, c0):
        for h in range(4):
            bank = h % 2
            self.proj_fm(slot, tok, c0 + h * 128, 128, self.ps[:, bank, :], ("ps", bank))
            self.headnorm(self.ps[:, bank, :], ("ps", bank), self.vecs[:, V_MQ + l:V_MQ + l + 1],
                          self.qmn[:, h, :], ("qmn", h))
        return

    def mem_attn2(self, l):
        sc = 128 ** -0.5
        for h in range(4):
            pbs = []
            for mc in range(2):
                bank = 2 + mc
                self.mm(self.ps[:, bank, :], self.memk[:, h, mc * 128:(mc + 1) * 128], self.qmn[:, h, :], True, True,
                        ["memk", ("qmn", h)], [("ps", bank)])
                b = self.pt_i % 2
                self.pt_i += 1
                self.act(self.pt[:, b, :], self.ps[:, bank, :], AF.Exp, [("ps", bank)], [("pt", b)], scale=sc)
                pbs.append(b)
            for mc in range(2):
                b = pbs[mc]
                self.mm(self.ps[:, 4, :], self.memv[:, mc, h * 128:(h + 1) * 128], self.pt[:, b, :], mc == 0, mc == 1,
                        ["memv", ("pt", b)], [("ps", 4)])
            for mc in range(2):
                b = pbs[mc]
                self.mm(self.ps[:, 5, :], self.ones[:], self.pt[:, b, :], mc == 0, mc == 1,
                        ["ones", ("pt", b)], [("ps", 5)])
            self.recip(self.stmp[:, 0, :], self.ps[:, 5, :], [("ps", 5)], [("stmp", 0)])
            self.tt(DVE, self.ht[:, 12 + h, :], self.ps[:, 4, :], self.stmp[:, 0, :], ALU.mult,
                    [("ps", 4), ("stmp", 0)], [("ht", 12 + h)])

    def mixer_a(self, l, t):
        self.rmsnorm_to_ht(V_MIX + l * DC)
        key = "win_%d" % l
        qscale = 192 ** -0.5
        sA, tA = self.load_slot(key, 0, 16, 0, 512)
        sB, tB = self.load_slot(key, 0, 16, 512, 256)
        pi = 0
        for h in range(4):
            for part, m in ((0, 128), (1, 64)):
                c = h * 192 + part * 128
                if c + m <= 512:
                    s_, t_, cc = sA, tA, c
                else:
                    s_, t_, cc = sB, tB, c - 512
                bank = pi % 4
                self.proj_fm(s_, t_, cc, m, self.ps[0:m, bank, :], ("ps", bank))
                self.act(self.qT[0:m, 2 * h + part, :], self.ps[0:m, bank, :], AF.Copy, [("ps", bank)],
                         [("qT", 2 * h + part)], scale=qscale)
                pi += 1
        sL, tL = self.load_slot(key, 0, 16, 3072, 16)
        self.proj_fm(sL, tL, 0, 16, self.ps[0:16, 4, :], ("ps", 4))
        self.cp(ACT, self.lrT[0:16, :], self.ps[0:16, 4, :], [("ps", 4)], ["lrT"])
        for vb in range(3):
            sV, tV = self.load_slot(key, 0, 16, 1536 + vb * 512, 512)
            for tg in range(4):
                bank = tg
                self.proj_tm(sV, tV, 0, 512, tg, self.ps[:, bank, :], ("ps", bank))
                self.cp(ACT if tg % 2 == 0 else DVE, self.vtm[:, tg * 3 + vb, :], self.ps[:, bank, :],
                        [("ps", bank)], [("vtm", tg)])
        sK0, tK0 = self.load_slot(key, 0, 16, 768, 512)
        sK1, tK1 = self.load_slot(key, 0, 16, 1280, 256)
        for tg in range(4):
            self.mm(self.ps[:, 4, :], self.lrT[0:17, tg * 128:(tg + 1) * 128], self.wup[0:17, l, 0:512], True, True,
                    ["lrT", "wup"], [("ps", 4)])
            self.mm(self.ps[:, 5, 0:256], self.lrT[0:17, tg * 128:(tg + 1) * 128], self.wup[0:17, l, 512:768], True, True,
                    ["lrT", "wup"], [("ps", 5)])
            self.act(self.ef[:, 0:512], self.ps[:, 4, :], AF.Exp, [("ps", 4)], ["ef"], scale=-1.0)
            self.act(self.ef[:, 512:768], self.ps[:, 5, 0:256], AF.Exp, [("ps", 5)], ["ef"], scale=-1.0)
            self.act(self.spf[:], self.ef[:], AF.Ln, ["ef"], ["spf"], bias=1.0)
            self.mm(self.ps[:, 4, :], self.mskD[:], self.spf[:, 0:512], True, True, ["mskD", "spf"], [("ps", 4)])
            self.mm(self.ps[:, 5, 0:256], self.mskD[:], self.spf[:, 512:768], True, True, ["mskD", "spf"], [("ps", 5)])
            for h in range(4):
                for part, m in ((0, 128), (1, 64)):
                    c = h * 192 + part * 128
                    p8 = 2 * h + part
                    self.mm(self.ps[0:m, 6, p8 * 8 + 2 * tg:p8 * 8 + 2 * tg + 2], self.spf[:, c:c + m], self.ind2[:],
                            True, True, ["spf", "ind2"], [("ps", 6)])
            self.act(self.ef[:, 0:512], self.ps[:, 4, :], AF.Exp, [("ps", 4)], ["ef"])
            self.act(self.ef[:, 512:768], self.ps[:, 5, 0:256], AF.Exp, [("ps", 5)], ["ef"])
            self.proj_tm(sK0, tK0, 0, 512, tg, self.ps[:, 0 + (tg % 2) * 2, :], ("ps", 0 + (tg % 2) * 2))
            self.proj_tm(sK1, tK1, 0, 256, tg, self.ps[:, 1 + (tg % 2) * 2, 0:256], ("ps", 1 + (tg % 2) * 2))
            self.tt(DVE, self._kdec_ap(tg, 0, 512),
                    self.ps[:, 0 + (tg % 2) * 2, :], self.ef[:, 0:512], ALU.mult,
                    [("ps", 0 + (tg % 2) * 2), "ef"], [("kdec", tg)])
            self.tt(DVE, self._kdec_ap(tg, 512, 256), self.ps[:, 1 + (tg % 2) * 2, 0:256], self.ef[:, 512:768],
                    ALU.mult, [("ps", 1 + (tg % 2) * 2), "ef"], [("kdec", tg)])
        self.act(self.acol[:].rearrange("p a b -> p (a b)"), self.ps[:, 6, 0:64], AF.Exp, [("ps", 6)], ["acol"])
        for gb in range(3):
            sG, tG = self.load_slot(key, 0, 16, 3088 + gb * 512, 512)
            for jj in range(4):
                bank = jj % 4
                self.proj_fm(sG, tG, jj * 128, 128, self.ps[:, bank, :], ("ps", bank))
                self.act(self.sg[:, gb * 4 + jj, :], self.ps[:, bank, :], AF.Silu, [("ps", bank)],
                         [("sg", gb * 4 + jj)])
        sM, tM = self.load_slot(key, 0, 16, 4624, 512)
        self.mem_attn(l, sM, tM, 0)
        self.mem_attn2(l)
        for h in range(4):
            for ch in range(8):
                tg, pbase = ch // 2, (ch % 2) * 64
                ba, bb = 3 + (ch % 2) * 2, 4 + (ch % 2) * 2
                kA = self._kdec_ap(tg, h * 192, 128)[pbase:pbase + 64, :]
                kB = self._kdec_ap(tg, h * 192 + 128, 64)[pbase:pbase + 64, :]
                vv = self._v_ap(tg, h * 384, 384)[pbase:pbase + 64, :]
                self.mm(self.ps[:, ba, 0:384], kA, vv, True, True, [("kdec", tg), ("vtm", tg)], [("ps", ba)])
                self.mm(self.ps[0:64, bb, 0:384], kB, vv, True, True, [("kdec", tg), ("vtm", tg)], [("ps", bb)])
                self.stt(DVE, self.SA[:, h, :], self.SA[:, h, :], self.acol[:, 2 * h, ch:ch + 1], self.ps[:, ba, 0:384],
                         ALU.mult, ALU.add, [("SA", h), "acol", ("ps", ba)], [("SA", h)])
                self.stt(DVE, self.SB[:, h, :], self.SB[:, h, :], self.acol[0:64, 2 * h + 1, ch:ch + 1],
                         self.ps[0:64, bb, 0:384], ALU.mult, ALU.add, [("SB", h), "acol", ("ps", bb)], [("SB", h)])
                sb_ = ch % 2
                self.cp(ACT, self.SAb[:, sb_, :], self.SA[:, h, :], [("SA", h)], [("SAb", sb_)])
                self.cp(ACT, self.SBb[:, sb_, :], self.SB[:, h, :], [("SB", h)], [("SBb", sb_)])
                for vc in range(3):
                    o_ap = self.ps[:, vc, ch * 64:(ch + 1) * 64]
                    self.mm(o_ap, self.SAb[:, sb_, vc * 128:(vc + 1) * 128], self.qT[:, 2 * h, ch * 64:(ch + 1) * 64],
                            True, False, [("SAb", sb_), ("qT", 2 * h)], [("ps", vc)])
                    self.mm(o_ap, self.SBb[:, sb_, vc * 128:(vc + 1) * 128],
                            self.qT[0:64, 2 * h + 1, ch * 64:(ch + 1) * 64],
                            False, True, [("SBb", sb_), ("qT", 2 * h + 1)], [("ps", vc)])
            for vc in range(3):
                b = self.sqbuf()
                self.act(self.sq[:, b, :], self.ps[:, vc, :], AF.Square, [("ps", vc)], [("sq", b)])
                self.mm(self.ps[:, 7, :], self.ones[:], self.sq[:, b, :], vc == 0, vc == 2, [("sq", b), "ones"],
                        [("ps", 7)])
            self.rstd_from(self.ps[:, 7, :], ("ps", 7), 1.0 / 384)
            for vc in range(3):
                b = self.st_i % 2
                self.st_i += 1
                self.stt(DVE, self.stmp[:, b, :], self.ps[:, vc, :], self.vecs[:, V_AOUT + l * 3 + vc:V_AOUT + l * 3 + vc + 1],
                         self.rstd[:], ALU.mult, ALU.mult, [("ps", vc), "rstd", "vecs"], [("stmp", b)])
                self.tt(DVE, self.ht[:, h * 3 + vc, :], self.stmp[:, b, :], self.sg[:, h * 3 + vc, :], ALU.mult,
                        [("stmp", b), ("sg", h * 3 + vc)], [("ht", h * 3 + vc)])

    def _kdec_ap(self, tg, c0, n):
        flat = self.gt[:, 36:42, :].rearrange("p a b -> p (a b)")
        return flat[:, tg * 768 + c0: tg * 768 + c0 + n]

    def _v_ap(self, tg, c0, n):
        flat = self.gt[:, 24:36, :].rearrange("p a b -> p (a b)")
        return flat[:, tg * 1536 + c0: tg * 1536 + c0 + n]

    def kv_shared(self, t):
        self.rmsnorm_to_ht(V_KVN)
        key = "wkv"
        for kb in range(3):
            s, tk = self.load_slot(key, 0, 16, kb * 512, 512)
            for jj in range(4):
                h = kb * 4 + jj
                bank = jj
                self.proj_fm(s, tk, jj * 128, 128, self.ps[:, bank, :], ("ps", bank))
                self.headnorm(self.ps[:, bank, :], ("ps", bank), self.vecs[:, V_KN:V_KN + 1], self.kst[:, h, :],
                              ("kst", h))
        self.dma(ACT, self.kT_d[:, :, t * T:(t + 1) * T].rearrange("h p s -> p h s"), self.kst,
                 [("kst", h) for h in range(12)], [("kTd", t)])
        for vb in range(3):
            s, tk = self.load_slot(key, 0, 16, 1536 + vb * 512, 512)
            for tg in range(4):
                bank = tg
                self.proj_tm(s, tk, 0, 512, tg, self.ps[:, bank, :], ("ps", bank))
                flat = self.gt[:, 12:24, :].rearrange("p a b -> p (a b)")
                self.cp(ACT if tg % 2 == 0 else DVE, flat[:, tg * 1536 + vb * 512: tg * 1536 + (vb + 1) * 512],
                        self.ps[:, bank, :], [("ps", bank)], [("vst", tg)])
        flat = self.gt[:, 12:24, :].rearrange("p a b -> p (a b)")
        self.dma(ACT, self.v_d[t * T:(t + 1) * T, :].rearrange("(g p) c -> p g c", p=128),
                 flat.rearrange("p (g c) -> p g c", g=4), [("vst", tg) for tg in range(4)], [("vd", t)])
        s, tk = self.load_slot(key, 0, 16, 3072, 12)
        for tg in range(4):
            i = t * 4 + tg
            self.proj_tm(s, tk, 0, 12, tg, self.ps[:, 4, 0:12], ("ps", 4))
            self.tt(DVE, self.fz[:], self.ps[:, 4, 0:12], self.rows[:, 0:12], ALU.add,
                    [("ps", 4), "rows"], ["fz"])
            self.act(self.fz[:], self.fz[:], AF.Exp, ["fz"], ["fz"], scale=-1.0)
            self.act(self.fsp[:], self.fz[:], AF.Ln, ["fz"], ["fsp"], bias=1.0)
            self.mm(self.ps[:, 5, 0:12], self.mtri[:], self.fsp[:], True, True, ["mtri", "fsp"], [("ps", 5)])
            self.mm(self.ps[:, 5, 12:24], self.mones[:], self.fsp[:], True, True, ["mones", "fsp"], [("ps", 5)])
            self.mm(self.ps[:, 5, 24:36], self.mhalf[:], self.fsp[:], True, True, ["mhalf", "fsp"], [("ps", 5)])
            self.tt(DVE, self.cT[:, i, :], self.ps[:, 5, 0:12], self.carry[:], ALU.add, [("ps", 5), "carry"], ["cT"])
            self.tt(DVE, self.cref[:, i, :], self.ps[:, 5, 24:36], self.carry[:], ALU.add, [("ps", 5), "carry"], ["cref"])
            self.tt(DVE, self.carry[:], self.ps[:, 5, 12:24], self.carry[:], ALU.add, [("ps", 5), "carry"], ["carry"])

    def _Qn(self, h):
        return self.gt[:, h, :] if h < 8 else self.gt[:, 24 + (h - 8), :]

    def mixer_b(self, l, t):
        j = l - 2
        self.rmsnorm_to_ht(V_MIX + l * DC)
        key = "win_%d" % l
        for qb in range(3):
            s, tk = self.load_slot(key, 0, 16, qb * 512, 512)
            for jj in range(4):
                h = qb * 4 + jj
                bank = jj
                self.proj_fm(s, tk, jj * 128, 128, self.ps[:, bank, :], ("ps", bank))
                self.headnorm(self.ps[:, bank, :], ("ps", bank), self.vecs[:, V_BQ + j:V_BQ + j + 1], self._Qn(h),
                              ("Qn", h))
        for gb in range(3):
            s, tk = self.load_slot(key, 0, 16, 1536 + gb * 512, 512)
            for jj in range(4):
                bank = jj
                self.proj_fm(s, tk, jj * 128, 128, self.ps[:, bank, :], ("ps", bank))
                self.act(self.sg[:, gb * 4 + jj, :], self.ps[:, bank, :], AF.Sigmoid, [("ps", bank)],
                         [("sg", gb * 4 + jj)])
        sM, tM = self.load_slot(key, 0, 16, 3072, 512)
        self.mem_attn(l, sM, tM, 0)
        self.mem_attn2(l)
        nk = (t + 1) * T
        nkc = nk // 128
        sc = 128 ** -0.5
        for hp in range(6):
            h0 = 2 * hp
            ksl, ktok = self.alloc_slot()
            kflat = ksl.rearrange("p a b -> p (a b)")
            self.dma(SP, kflat.rearrange("p (h s) -> p h s", h=2)[:, :, 0:nk],
                     self.kT_d[h0:h0 + 2, :, 0:nk].rearrange("h p s -> p h s"),
                     [("kTd", tt_) for tt_ in range(t + 1)], [ktok])
            vsl, vtok = self.alloc_slot()
            vbuf = vsl.rearrange("p a b -> p (a b)").rearrange("p (c d) -> p c d", d=256)
            self.dma(SP, vbuf[:, 0:nkc, :],
                     self.v_d[0:nk, h0 * 128:h0 * 128 + 256].rearrange("(c p) d -> p c d", p=128),
                     [("vd", tt_) for tt_ in range(t + 1)], [vtok])
            for hh in range(2):
                h = h0 + hh
                for qb in range(4):
                    i = t * 4 + qb
                    bb_ = (h * 4 + qb) % 2
                    self.ts(DVE, self.biasb[:, bb_, 0:i + 1], self.cT[:, 0:i + 1, h], self.cref[:, i, h:h + 1], -1.0,
                            ALU.subtract, ALU.mult, ["cT", "cref"], [("biasb", bb_)])
                    bo, bd = 4 + bb_, 6 + bb_
                    for kc in range(i + 1):
                        bs = self.pb_i % 4
                        self.pb_i += 1
                        self.mm(self.ps[:, bs, 0:128], kflat[:, hh * 4096 + kc * 128: hh * 4096 + (kc + 1) * 128],
                                self._Qn(h)[:, qb * 128:(qb + 1) * 128], True, True, [ktok, ("Qn", h)], [("ps", bs)])
                        self.act(self.pb[:, bs, :], self.ps[:, bs, 0:128], AF.Exp, [("ps", bs), ("biasb", bb_)],
                                 [("pb", bs)], bias=self.biasb[:, bb_, kc:kc + 1], scale=sc)
                        if kc == i:
                            self.tt(DVE, self.pb[:, bs, :], self.pb[:, bs, :], self.cmask[:], ALU.mult,
                                    [("pb", bs), "cmask"], [("pb", bs)])
                        self.mm(self.ps[:, bo, 0:128], vbuf[:, kc, hh * 128:(hh + 1) * 128], self.pb[:, bs, :],
                                kc == 0, kc == i, [vtok, ("pb", bs)], [("ps", bo)])
                        self.mm(self.ps[:, bd, 0:128], self.ones[:], self.pb[:, bs, :], kc == 0, kc == i,
                                ["ones", ("pb", bs)], [("ps", bd)])
                    self.recip(self.rcp[:, bb_, :], self.ps[:, bd, 0:128], [("ps", bd)], [("rcp", bb_)])
                    self.tt(DVE, self.rcp[:, bb_, :], self.ps[:, bo, 0:128], self.rcp[:, bb_, :], ALU.mult,
                            [("ps", bo), ("rcp", bb_)], [("rcp", bb_)])
                    self.tt(DVE, self.ht[:, h, qb * 128:(qb + 1) * 128], self.rcp[:, bb_, :],
                            self.sg[:, h, qb * 128:(qb + 1) * 128], ALU.mult, [("rcp", bb_), ("sg", h)], [("ht", h)])


V_FFN = 0
V_MIX = V_FFN + 128
V_MEMN = V_MIX + 64
V_KVN = V_MEMN + 64
V_MQ = V_KVN + 16
V_MK = V_MQ + 4
V_AOUT = V_MK + 4
V_BQ = V_AOUT + 6
V_KN = V_BQ + 2
NV = V_KN + 1

R_BF = 0
R_MSKD = 12
R_IND2 = R_MSKD + 128
R_TRI = R_IND2 + 2
R_ONES = R_TRI + 128
R_HALF = R_ONES + 128
R_CMASK = R_HALF + 128
NR = R_CMASK + 128


def pack_vecs(inp):
    v = np.zeros((128, NV), np.float32)

    def cols(a):
        a = np.asarray(a, np.float32)
        return a.reshape(-1, 128).T

    v[:, V_FFN:V_FFN + 128] = cols(inp["ffn_norm"])
    v[:, V_MIX:V_MIX + 64] = cols(inp["mix_norm"])
    v[:, V_MEMN:V_MEMN + 64] = cols(inp["mem_norm"])
    v[:, V_KVN:V_KVN + 16] = cols(inp["kv_norm"])
    v[:, V_MQ:V_MQ + 4] = cols(inp["mem_q_norm"])
    v[:, V_MK:V_MK + 4] = cols(inp["mem_k_norm"])
    v[:, V_AOUT:V_AOUT + 6] = cols(inp["a_out_norm"])
    v[:, V_BQ:V_BQ + 2] = cols(inp["b_q_norm"])
    v[:, V_KN:V_KN + 1] = cols(inp["k_norm"])
    return v


def pack_rows(inp):
    r = np.zeros((128, NR), np.float32)
    r[:, R_BF:R_BF + 12] = np.asarray(inp["b_f"], np.float32)[None, :]
    s_ = np.arange(128)[:, None]
    t_ = np.arange(128)[None, :]
    r[:, R_MSKD:R_MSKD + 128] = np.where((s_ > t_) & (s_ // 64 == t_ // 64), -1.0 / 16, 0.0)
    r[:, R_IND2:R_IND2 + 2] = np.where(s_ // 64 == np.arange(2)[None, :], -1.0 / 16, 0.0)
    r[:, R_TRI:R_TRI + 128] = np.where(s_ <= t_, -1.0, 0.0)
    r[:, R_ONES:R_ONES + 128] = -1.0
    r[:, R_HALF:R_HALF + 128] = np.where(s_ <= 64, -1.0, 0.0) + 0.0 * t_
    r[:, R_CMASK:R_CMASK + 128] = np.where(s_ <= t_, 1.0, 0.0)
    return r


def pack_wup(inp):
    w = np.zeros((2, 17, GLA_K), np.float32)
    w[:, 0:16, :] = np.asarray(inp["a_w_gate_up"], np.float32)
    w[:, 16, :] = np.asarray(inp["a_b_gate"], np.float32)
    return w


def weight_map(inp, keys):
    m = {}
    for key in keys:
        p = key.split("_")
        if p[0] in ("w1", "w3", "w2"):
            m[key] = np.ascontiguousarray(inp["ffn_" + p[0]][int(p[1]), int(p[2])], dtype=np.float32)
        elif p[0] == "wkv":
            m[key] = np.ascontiguousarray(inp["w_kv"], dtype=np.float32)
        elif p[0] == "wmem":
            m[key] = np.ascontiguousarray(inp["w_mem_kv"][int(p[1])], dtype=np.float32)
        elif p[0] == "wout":
            m[key] = np.ascontiguousarray(inp["w_out"][int(p[1])], dtype=np.float32)
        elif p[0] == "win":
            l = int(p[1])
            m[key] = np.ascontiguousarray(inp["a_w_in"][l] if l < 2 else inp["b_w_in"][l - 2], dtype=np.float32)
    return m


def run_cores(inp, S, layers=(0, 1, 2, 3), parts=("ffn0", "mix", "ffn1"), n_cores=8, trace=False):
    kb = K(S, layers=layers, parts=parts)
    nc = kb.build()
    shared = weight_map(inp, kb.w_shapes.keys())
    shared["vecs"] = pack_vecs(inp)
    shared["rows"] = pack_rows(inp)
    shared["wup"] = pack_wup(inp)
    x = np.asarray(inp["x"], np.float32)
    mem = np.asarray(inp["mem"], np.float32)
    in_maps = []
    for b in range(n_cores):
        m = dict(shared)
        m["xT"] = np.ascontiguousarray(x[b, :S].T)
        m["memT"] = np.ascontiguousarray(mem[b].T)
        in_maps.append(m)
    res = run_bass_kernel_spmd(nc, in_maps, core_ids=list(range(n_cores)), trace=trace)
    out = np.stack([np.asarray(r["outT"]).T for r in res.results], axis=0)
    return out, res, kb


def kernel(**inputs):
    inp = {k: np.asarray(v) for k, v in inputs.items()}
    out, _, _ = run_cores(inp, 4096)
    return np.ascontiguousarray(out, dtype=np.float32)
```
